# Optimizing a Trainium2 kernel written in Bass

```python
import jax
import jax.numpy as jnp
from jax import lax
import numpy as np

D_MODEL = 1024
BATCH = 4
SEQ = 4096
DEPTH = 2

GRID_W = 64
CTX_LEN = 256
N_MOD = 9
D_FF = 2816
RMS_EPS = 1e-6
A_HEADS = 8
A_KV_HEADS = 2
A_HEAD_DIM = 64
WINDOW = 128
BLOCK = WINDOW
ROPE_BASE = 10000.0
B_HEADS = 4
B_DK = 64
B_DV = 128
B_GATE_RANK = 16
B_GATE_NORM = 16.0
B_CHUNK = 64
POOL_WINDOWS = (2, 4, 8, 16)
POOL_GROUP = D_MODEL // len(POOL_WINDOWS)
A_Q = A_HEADS * A_HEAD_DIM
A_KV = A_KV_HEADS * A_HEAD_DIM
B_QK = B_HEADS * B_DK
B_V = B_HEADS * B_DV
PROJ_SIZES = (A_Q, A_KV, A_KV, B_QK, B_QK, B_V, B_V, 2 * B_GATE_RANK)
PROJ_DIM = A_Q + 2 * A_KV + 2 * B_QK + 2 * B_V + 2 * B_GATE_RANK
MIX_OUT = A_Q + B_V

kernel_name = "hybrid_swa_gla_pool_prefix_dit"


def rmsnorm(x, g):
    x32 = x.astype(jnp.float32)
    y = x32 * lax.rsqrt(jnp.mean(x32 * x32, axis=-1, keepdims=True) + RMS_EPS)
    return y.astype(x.dtype) * g


def adaln(cond, w, b):
    mm = jax.nn.silu(cond) @ w + b
    mm = mm.reshape(mm.shape[:-1] + (N_MOD, mm.shape[-1] // N_MOD))
    return [mm[..., i, None, :] for i in range(N_MOD)]


def modulate(z, g, shift, scale):
    return rmsnorm(z, g) * (1.0 + scale) + shift


def swiglu(h, wi, wo):
    a, u = jnp.split(h @ wi, 2, axis=-1)
    return (jax.nn.silu(a) * u) @ wo


def _rotate(x, pos):
    n = x.shape[-1] // 2
    freqs = ROPE_BASE ** (-jnp.arange(n, dtype=jnp.float32) / n)
    ang = pos[:, None] * freqs
    cos = jnp.cos(ang)[:, None, :].astype(x.dtype)
    sin = jnp.sin(ang)[:, None, :].astype(x.dtype)
    x1, x2 = x[..., :n], x[..., n:]
    return jnp.concatenate([x1 * cos - x2 * sin, x2 * cos + x1 * sin], axis=-1)


def axial_rope(x, rows, cols):
    half = x.shape[-1] // 2
    return jnp.concatenate([_rotate(x[..., :half], rows), _rotate(x[..., half:], cols)], axis=-1)


def window_attention(q, k, v, kc, vc, sink):
    B, T, Hq, d = q.shape
    G = k.shape[2]
    R = Hq // G
    nb = T // BLOCK
    L = kc.shape[1]
    scale = d ** -0.5
    qb = q.reshape(B, nb, BLOCK, G, R, d)

    def band(a):
        ap = jnp.pad(a, ((0, 0), (BLOCK, BLOCK), (0, 0), (0, 0))).reshape(B, nb + 2, BLOCK, G, d)
        return jnp.concatenate([ap[:, :nb], ap[:, 1:nb + 1], ap[:, 2:]], axis=2)

    kb, vb = band(k), band(v)
    s_band = jnp.einsum('bnigrd,bnjgd->bgrnij', qb, kb).astype(jnp.float32) * scale
    s_ctx = jnp.einsum('bnigrd,blgd->bgrnil', qb, kc).astype(jnp.float32) * scale
    qpos = jnp.arange(nb)[:, None, None] * BLOCK + jnp.arange(BLOCK)[None, :, None]
    kpos = jnp.arange(nb)[:, None, None] * BLOCK - BLOCK + jnp.arange(3 * BLOCK)[None, None, :]
    valid = (kpos >= 0) & (kpos < T) & (jnp.abs(kpos - qpos) <= WINDOW)
    s_band = jnp.where(valid, s_band, -jnp.inf)
    sink_l = jnp.broadcast_to(sink.astype(jnp.float32).reshape(1, G, R, 1, 1, 1), s_band.shape[:-1] + (1,))
    p = jax.nn.softmax(jnp.concatenate([s_band, s_ctx, sink_l], axis=-1), axis=-1)
    nk = 3 * BLOCK
    o = (jnp.einsum('bgrnij,bnjgd->bnigrd', p[..., :nk].astype(vb.dtype), vb)
         + jnp.einsum('bgrnil,blgd->bnigrd', p[..., nk:nk + L].astype(vc.dtype), vc))
    return o.reshape(B, T, Hq * d)


def context_attention(qc, kc, vc, sink):
    B, L, Hq, d = qc.shape
    G = kc.shape[2]
    R = Hq // G
    s = jnp.einsum('blgrd,bmgd->bgrlm', qc.reshape(B, L, G, R, d), kc).astype(jnp.float32) * d ** -0.5
    sink_l = jnp.broadcast_to(sink.astype(jnp.float32).reshape(1, G, R, 1, 1), s.shape[:-1] + (1,))
    p = jax.nn.softmax(jnp.concatenate([s, sink_l], axis=-1), axis=-1)
    o = jnp.einsum('bgrlm,bmgd->blgrd', p[..., :L].astype(vc.dtype), vc)
    return o.reshape(B, L, Hq * d)


def gla_chunked(q, k, v, log_a, s0):
    B, T, H, dk = q.shape
    C = B_CHUNK
    n = T // C
    f32 = jnp.float32

    def chunks(a):
        return a.astype(f32).reshape(B, n, C, H, a.shape[-1])

    qc_ = chunks(q) * dk ** -0.5
    kc_ = chunks(k)
    vc_ = chunks(v)
    g = jnp.cumsum(chunks(log_a), axis=2)
    g_last = g[:, :, -1:]
    q_t = qc_ * jnp.exp(g)
    k_t = kc_ * jnp.exp(-g)
    k_end = kc_ * jnp.exp(g_last - g)
    lower = jnp.tril(jnp.ones((C, C), dtype=bool))
    att = jnp.where(lower, jnp.einsum('bnihd,bnjhd->bnhij', q_t, k_t), 0.0)
    o = jnp.einsum('bnhij,bnjhv->bnihv', att, vc_)
    d_state = jnp.einsum('bnjhd,bnjhv->bnhdv', k_end, vc_)
    decay = jnp.exp(g_last[:, :, 0])

    def step(S, inp):
        dec, ds = inp
        return dec[..., None] * S + ds, S

    s_final, s_prev = lax.scan(step, s0, (jnp.moveaxis(decay, 1, 0), jnp.moveaxis(d_state, 1, 0)))
    s_prev = jnp.moveaxis(s_prev, 0, 1)
    o = o + jnp.einsum('bnihd,bnhdv->bnihv', q_t, s_prev)
    return o.reshape(B, T, H, v.shape[-1]).astype(q.dtype), s_final


def bidir_gla(q, k, v, la_f, la_b, qc, kc, vc, lac_f, lac_b):
    flip = lambda a: a[:, ::-1]
    B = q.shape[0]
    zeros = jnp.zeros((B, B_HEADS, B_DK, B_DV), jnp.float32)
    oc_f, sc_f = gla_chunked(qc, kc, vc, lac_f, zeros)
    oc_b, sc_b = gla_chunked(flip(qc), flip(kc), flip(vc), flip(lac_b), zeros)
    o_f, _ = gla_chunked(q, k, v, la_f, sc_f)
    o_b, _ = gla_chunked(flip(q), flip(k), flip(v), flip(la_b), sc_b)
    return o_f + flip(o_b), oc_f + flip(oc_b)


def gla_output(o, r, gla_g):
    B, T = o.shape[:2]
    return rmsnorm(o, gla_g).reshape(B, T, B_V) * jax.nn.silu(r)


def mixer_ab(h, hc, rows, cols, need_ctx_out, w_in, w_a2_f, b_a_f, w_a2_b, b_a_b, sink, gla_g, w_out):
    split_points = [int(s) for s in np.cumsum(PROJ_SIZES)[:-1]]

    def project(z):
        Bz, Tz = z.shape[:2]
        qa, ka, va, qb, kb, vb, rb, zg = jnp.split(z @ w_in, split_points, axis=-1)
        la_f = jax.nn.log_sigmoid((zg[..., :B_GATE_RANK] @ w_a2_f + b_a_f).astype(jnp.float32)) / B_GATE_NORM
        la_b = jax.nn.log_sigmoid((zg[..., B_GATE_RANK:] @ w_a2_b + b_a_b).astype(jnp.float32)) / B_GATE_NORM
        return (qa.reshape(Bz, Tz, A_HEADS, A_HEAD_DIM),
                ka.reshape(Bz, Tz, A_KV_HEADS, A_HEAD_DIM),
                va.reshape(Bz, Tz, A_KV_HEADS, A_HEAD_DIM),
                qb.reshape(Bz, Tz, B_HEADS, B_DK),
                kb.reshape(Bz, Tz, B_HEADS, B_DK),
                vb.reshape(Bz, Tz, B_HEADS, B_DV),
                rb,
                la_f.reshape(Bz, Tz, B_HEADS, B_DK),
                la_b.reshape(Bz, Tz, B_HEADS, B_DK))

    qa, ka, va, qb, kb, vb, rb, la_f, la_b = project(h)
    cqa, cka, cva, cqb, ckb, cvb, crb, cla_f, cla_b = project(hc)
    o_a = window_attention(axial_rope(qa, rows, cols), axial_rope(ka, rows, cols), va, cka, cva, sink)
    o_b, oc_b = bidir_gla(qb, kb, vb, la_f, la_b, cqb, ckb, cvb, cla_f, cla_b)
    y = jnp.concatenate([o_a, gla_output(o_b, rb, gla_g)], axis=-1) @ w_out
    yc = None
    if need_ctx_out:
        oc_a = context_attention(cqa, cka, cva, sink)
        yc = jnp.concatenate([oc_a, gla_output(oc_b, crb, gla_g)], axis=-1) @ w_out
    return y, yc


def pool_mixer(h, w_pool, pool_scale):
    B, T, D = h.shape
    ng = len(POOL_WINDOWS)
    hg = h.astype(jnp.float32).reshape(B, T, ng, POOL_GROUP)
    prefix = jnp.pad(jnp.cumsum(hg, axis=1), ((0, 0), (1, 0), (0, 0), (0, 0)))
    t = jnp.arange(T)
    means = []
    for gi, w in enumerate(POOL_WINDOWS):
        lo = jnp.maximum(t - w // 2, 0)
        hi = jnp.minimum(t + (w - w // 2), T)
        total = prefix[:, hi, gi] - prefix[:, lo, gi]
        means.append(total / (hi - lo).astype(jnp.float32)[:, None])
    pooled = (jnp.stack(means, axis=2) - hg).astype(h.dtype)
    y = jnp.einsum('btgc,gce->btge', pooled, w_pool)
    return y.reshape(B, T, D) * pool_scale


def setup_inputs(seed: int = 0) -> dict:
    key = jax.random.key(seed)
    ks = jax.random.split(key, 24)
    D = D_MODEL
    ne = (DEPTH + 1) // 2
    no = DEPTH // 2

    def nrm(k, shape, scale=1.0):
        return jax.random.normal(k, shape, jnp.float32) * scale

    return {
        "x": nrm(ks[0], (BATCH, SEQ, D)),
        "c": nrm(ks[1], (BATCH, D)),
        "ctx": nrm(ks[2], (BATCH, CTX_LEN, D)),
        "c_ctx": nrm(ks[3], (D,)),
        "w_mod": nrm(ks[4], (DEPTH, D, N_MOD * D), 0.5 * D ** -0.5),
        "b_mod": nrm(ks[5], (DEPTH, N_MOD * D), 0.01),
        "norm_g": 1.0 + nrm(ks[6], (DEPTH, 3, D), 0.05),
        "ffn1_wi": nrm(ks[7], (DEPTH, D, 2 * D_FF), D ** -0.5),
        "ffn1_wo": nrm(ks[8], (DEPTH, D_FF, D), D_FF ** -0.5),
        "ffn2_wi": nrm(ks[9], (DEPTH, D, 2 * D_FF), D ** -0.5),
        "ffn2_wo": nrm(ks[10], (DEPTH, D_FF, D), D_FF ** -0.5),
        "w_in": nrm(ks[11], (ne, D, PROJ_DIM), D ** -0.5),
        "w_a2_f": nrm(ks[12], (ne, B_GATE_RANK, B_QK), B_GATE_RANK ** -0.5),
        "b_a_f": nrm(ks[13], (ne, B_QK), 0.1),
        "w_a2_b": nrm(ks[14], (ne, B_GATE_RANK, B_QK), B_GATE_RANK ** -0.5),
        "b_a_b": nrm(ks[15], (ne, B_QK), 0.1),
        "sink": nrm(ks[16], (ne, A_HEADS), 1.0),
        "gla_g": 1.0 + nrm(ks[17], (ne, B_DV), 0.05),
        "w_out": nrm(ks[18], (ne, MIX_OUT, D), MIX_OUT ** -0.5),
        "w_pool": nrm(ks[19], (no, len(POOL_WINDOWS), POOL_GROUP, POOL_GROUP), POOL_GROUP ** -0.5),
        "pool_scale": 1.0 + nrm(ks[20], (no, D), 0.1),
        "final_g": 1.0 + nrm(ks[21], (D,), 0.05),
    }


def reference(x, c, ctx, c_ctx, w_mod, b_mod, norm_g, ffn1_wi, ffn1_wo, ffn2_wi, ffn2_wo,
              w_in, w_a2_f, b_a_f, w_a2_b, b_a_b, sink, gla_g, w_out, w_pool, pool_scale, final_g):
    T = x.shape[1]
    ROWS = T // GRID_W
    rows = jnp.repeat(jnp.arange(ROWS, dtype=jnp.float32), GRID_W)
    cols = jnp.tile(jnp.arange(GRID_W, dtype=jnp.float32), ROWS)

    for l in range(DEPTH):
        even = l % 2 == 0
        ctx_out = any(j % 2 == 0 for j in range(l + 1, DEPTH))
        ctx_in = even or ctx_out
        m = adaln(c, w_mod[l], b_mod[l])
        mc = adaln(c_ctx, w_mod[l], b_mod[l]) if ctx_in else None

        x = x + 0.5 * m[2] * swiglu(modulate(x, norm_g[l, 0], m[0], m[1]), ffn1_wi[l], ffn1_wo[l])
        if ctx_in:
            ctx = ctx + 0.5 * mc[2] * swiglu(modulate(ctx, norm_g[l, 0], mc[0], mc[1]), ffn1_wi[l], ffn1_wo[l])

        h = modulate(x, norm_g[l, 1], m[3], m[4])
        if even:
            e = l // 2
            hc = modulate(ctx, norm_g[l, 1], mc[3], mc[4])
            y, yc = mixer_ab(h, hc, rows, cols, ctx_out, w_in[e], w_a2_f[e], b_a_f[e], w_a2_b[e], b_a_b[e],
                             sink[e], gla_g[e], w_out[e])
        else:
            o = l // 2
            y = pool_mixer(h, w_pool[o], pool_scale[o])
            yc = pool_mixer(modulate(ctx, norm_g[l, 1], mc[3], mc[4]), w_pool[o], pool_scale[o]) if ctx_out else None
        x = x + m[5] * y
        if ctx_out:
            ctx = ctx + mc[5] * yc

        x = x + 0.5 * m[8] * swiglu(modulate(x, norm_g[l, 2], m[6], m[7]), ffn2_wi[l], ffn2_wo[l])
        if ctx_out:
            ctx = ctx + 0.5 * mc[8] * swiglu(modulate(ctx, norm_g[l, 2], mc[6], mc[7]), ffn2_wi[l], ffn2_wo[l])

    return rmsnorm(x, final_g)
```

```python
from contextlib import ExitStack
import numpy as np
import concourse.bass as bass
import concourse.mybir as mybir
from concourse.bass_utils import run_bass_kernel_spmd

F32 = mybir.dt.float32
BF16 = mybir.dt.bfloat16
AF = mybir.ActivationFunctionType
ALU = mybir.AluOpType

D = 1024
NDC = 8
DFF = 2816
NF = 22
SEQ = 4096
NOWN = 2048
NHALO = 128
NCTX = 256
NTOT = NOWN + NHALO + NCTX
NKA = NOWN + NHALO + NCTX
EPS = 1e-6
ARENA_BYTES = 131072
NWP = 22
C_QA, C_QAP, C_KA, C_KAP, C_QB, C_KB, C_RB, C_ZGX, C_ZGY = 0, 4, 8, 10, 12, 14, 16, 20, 21


class Dep:
    __slots__ = ("w", "r")

    def __init__(self):
        self.w = None
        self.r = {}


class _Eng:
    def __init__(self, name, sem):
        self.name = name
        self.sem = sem
        self.count = 0
        self.known = {}
        self.prog = []
        self.pending = []


class MK:
    ENGS = ("tensor", "vector", "scalar", "gpsimd", "sync")

    def __init__(self, nc, n_dma_sems=28):
        self.nc = nc
        self.es = ExitStack()
        self.engs = {}
        for name in self.ENGS:
            sem = self.es.enter_context(nc.semaphore("s_" + name))
            self.engs[name] = _Eng(name, sem)
        self.dsems = []
        for i in range(n_dma_sems):
            self.dsems.append([self.es.enter_context(nc.semaphore("d%d" % i)), 0])
        self.dpool = {"sync": self.dsems[:18], "gpsimd": self.dsems[18:]}
        self.dma_i = {"sync": 0, "gpsimd": 0}

    def sb(self, name, shape, dtype):
        return self.es.enter_context(self.nc.sbuf_tensor(name, list(shape), dtype))

    def ps(self, name, shape, dtype=F32):
        return self.es.enter_context(self.nc.psum_tensor(name, list(shape), dtype))

    def _collect(self, E, reads, writes, extra=()):
        need = {}

        def add(tok):
            if tok is None:
                return
            sem, val = tok
            k = id(sem)
            if sem is E.sem and E.name == "tensor":
                return
            if E.known.get(k, 0) >= val:
                return
            if k not in need or need[k][1] < val:
                need[k] = (sem, val)

        for d in reads:
            add(d.w)
        for d in writes:
            add(d.w)
            for t in d.r.values():
                add(t)
        for t in extra:
            add(t)
        for t in E.pending:
            add(t)
        E.pending = []
        for k, (sem, val) in need.items():
            E.known[k] = val
        return list(need.values())

    @staticmethod
    def _mark(tok, reads, writes):
        k = id(tok[0])
        for d in reads:
            t = d.r.get(k)
            if t is None or t[1] < tok[1]:
                d.r[k] = tok
        for d in writes:
            d.w = tok
            d.r = {}

    def op(self, eng, fn, reads=(), writes=(), extra=()):
        E = self.engs[eng]
        wl = self._collect(E, reads, writes, extra)
        E.count += 1
        tok = (E.sem, E.count)
        sem = E.sem

        def emit(e):
            for s, v in wl:
                e.wait_ge(s, v)
            fn(e).then_inc(sem, 1)

        E.prog.append(emit)
        self._mark(tok, reads, writes)
        return tok

    def dma(self, queue, out, in_, reads=(), writes=(), **kw):
        E = self.engs[queue]
        pool = self.dpool[queue]
        slot = pool[self.dma_i[queue] % len(pool)]
        self.dma_i[queue] += 1
        extra = [(slot[0], slot[1])] if slot[1] > 0 else []
        wl = self._collect(E, reads, writes, extra)
        slot[1] += 16
        tok = (slot[0], slot[1])
        sem = slot[0]

        def emit(e):
            for s, v in wl:
                e.wait_ge(s, v)
            e.dma_start(out=out, in_=in_, **kw).then_inc(sem, 16)

        E.prog.append(emit)
        self._mark(tok, reads, writes)
        return tok

    def wait_all(self, eng, deps):
        E = self.engs[eng]
        wl = self._collect(E, deps, ())

        def emit(e):
            for s, v in wl:
                e.wait_ge(s, v)

        E.prog.append(emit)

    def barrier(self, dummy_ap):
        toks = [(E.sem, E.count) for E in self.engs.values() if E.count > 0]
        toks += [(s, v) for s, v in self.dsems if v > 0]
        tok = self.op("vector", lambda e: e.memset(dummy_ap, 0.0), extra=toks)
        for E in self.engs.values():
            if E.name != "vector":
                E.pending.append(tok)

    def build(self):
        nc = self.nc
        with nc.Block() as block:
            for name in self.ENGS:
                prog = self.engs[name].prog
                if not prog:
                    continue

                def body(e, prog=prog):
                    for f in prog:
                        f(e)

                getattr(block, name)(body)
        self.es.close()


class Rot:
    def __init__(self, items):
        self.items = items
        self.i = 0

    def get(self):
        it = self.items[self.i % len(self.items)]
        self.i += 1
        return it


_DT_SIZE = {F32: 4, BF16: 2}


class Arena:
    def __init__(self, m, nbytes):
        self.t = m.sb("arena", [128, nbytes // 4], F32)
        self.nbytes = nbytes
        self.off = 0

    def reset(self, off=0):
        self.off = off

    def take(self, free_shape, dtype):
        n = 1
        for v in free_shape:
            n *= v
        nb = (n * _DT_SIZE[dtype] + 31) // 32 * 32
        assert self.off + nb <= self.nbytes, ("arena overflow", self.off, nb)
        ap = self.t[:, self.off // 4:(self.off + nb) // 4]
        if dtype != F32:
            ap = ap.bitcast(dtype)
        ap = ap[:, 0:n]
        if len(free_shape) == 2:
            ap = ap.rearrange("p (a b) -> p a b", b=free_shape[1])
        elif len(free_shape) == 3:
            ap = ap.rearrange("p (a b c) -> p a b c", b=free_shape[1], c=free_shape[2])
        self.off += nb
        return ap

    def rot(self, k, free_shape, dtype):
        return Rot([(self.take(free_shape, dtype), Dep()) for _ in range(k)])


def build_program(stop=None, debug=False, ncc=8, flags=()):
    nc = bass.Bass("TRN2", target_bir_lowering=False)
    m = MK(nc)
    dbg_deps = []

    def din(name, shape, dt=F32):
        return nc.dram_tensor(name, list(shape), dt, kind="ExternalInput").ap()

    def dscr(name, shape, dt):
        return nc.dram_tensor(name, list(shape), dt).ap()

    xin = din("xin", [128, NDC, NTOT])
    cv = din("cv", [128, NDC, 2])
    wmod = din("wmod", [2, 72, 128, NDC * 128])
    bmod = din("bmod", [128, 2, 72])
    ng = din("ng", [128, 2, 3, NDC])
    fg = din("fg", [128, NDC])
    if stop == "mixonly":
        wi_d = wo_d = None
    else:
        wi_d = din("wi", [2, 2, NF, 128, NDC * 256])
        wo_d = din("wo", [2, 2, NDC, 128, NF * 128])
    wp_d = din("wp", [NWP, 128, NDC * 128])
    wva_d = din("wva", [128, NDC * 128])
    wvb_d = din("wvb", [128, NDC * 512])
    wkbt_d = din("wkbt", [128, NDC * 256])
    cos_d = din("cosT", [128, NOWN + NHALO])
    sin_d = din("sinT", [128, NOWN + NHALO])
    mP_d = din("mP", [128, 512])
    mN_d = din("mN", [128, 512])
    tri_d = din("tri", [128, 2, 128])
    sink_d = din("sinkb", [128, 8])
    wa2_d = din("wa2", [16, 2, 256])
    ba_d = din("ba", [128, 2, 256])
    glag_d = din("glag", [128, 1])
    wouta_d = din("wouta", [128, 4, D])
    woutb_d = din("woutb", [128, 4, D])
    wpool_d = din("wpool", [128, 4, 2, 256])
    pscale_d = din("pscale", [128, NDC])
    psel_d = din("psel", [128, ncc])
    poolw_d = din("poolw", [128, NDC, 2])
    poolc_d = din("poolc", [128, NDC, 8])
    out_d = nc.dram_tensor("out", [128, NDC, NOWN], F32, kind="ExternalOutput").ap()

    QA = dscr("s_QA", [4, 128, NOWN], BF16)
    KA = dscr("s_KA", [2, 128, NKA], BF16)
    VA = dscr("s_VA", [19, 128, 128], BF16)
    QB = dscr("s_QB", [2, 128, NOWN], BF16)
    KB = dscr("s_KB", [2, 128, NOWN + NCTX], BF16)
    KBT = dscr("s_KBT", [18, 128, 256], BF16)
    VB = dscr("s_VB", [18, 128, 512], BF16)
    ZG = dscr("s_ZG", [2, 128, NOWN + NCTX], BF16)
    SRB = dscr("s_SRB", [4, 128, NOWN], F32)
    OX = dscr("s_OX", [16, 128, 512], F32)
    OB = dscr("s_OB", [4, 128, NOWN], BF16)
    OA = dscr("s_OA", [4, 128, NOWN], BF16)
    cc1_src = dscr("cc1_src", [128, 256], F32)
    cc1_dst = dscr("cc1_dst", [ncc * 128, 256], F32)
    cc2_src = dscr("cc2_src", [128, 64], F32)
    cc2_dst = dscr("cc2_dst", [ncc * 128, 64], F32)

    xT = m.sb("xT", [128, NDC, NOWN], F32)
    dx = [[Dep() for _ in range(4)] for _ in range(NDC)]
    ones = m.sb("ones", [128, 128], BF16)
    d_ones = Dep()
    epsT = m.sb("epsT", [128, 1], F32)
    d_eps = Dep()
    cvs = m.sb("cvs", [128, NDC, 2], F32)
    cvb = m.sb("cvb", [128, NDC, 2], BF16)
    d_cv = Dep()
    bmodT = m.sb("bmodT", [128, 2, 72], F32)
    ngT = m.sb("ngT", [128, 2, 3, NDC], F32)
    fgT = m.sb("fgT", [128, NDC], F32)
    pselT = m.sb("pselT", [128, ncc], F32)
    pscT = m.sb("pscT", [128, NDC], F32)
    glagT = m.sb("glagT", [128, 1], F32)
    d_small = Dep()
    modv = [m.sb("modv%d" % l, [128, 72, 2], F32) for l in range(2)]
    d_modv = [Dep(), Dep()]
    gsv = [m.sb("gsv%d" % l, [128, 3, NDC, 2], F32) for l in range(2)]
    hgv = [m.sb("hgv%d" % l, [128, 3, NDC, 2], F32) for l in range(2)]
    d_gs = [Dep(), Dep()]
    dummy = m.sb("bar_dummy", [128, 8], F32)
    arena = Arena(m, ARENA_BYTES)

    banks = Rot([(m.ps("bank%d" % i, [128, 512])[:], Dep()) for i in range(8)])

    m.op("vector", lambda e: e.memset(ones[:], 1.0), writes=[d_ones])
    m.op("vector", lambda e: e.memset(epsT[:], EPS), writes=[d_eps])
    m.dma("sync", cvs[:], cv, writes=[d_cv])
    for dst, src in ((bmodT, bmod), (ngT, ng), (fgT, fg), (pselT, psel_d), (pscT, pscale_d), (glagT, glag_d)):
        m.dma("sync", dst[:], src, writes=[d_small])
    for dc in range(NDC):
        for q in range(4):
            m.dma("sync", xT[:, dc, q * 512:(q + 1) * 512], xin[:, dc, q * 512:(q + 1) * 512], writes=[dx[dc][q]])

    xe = arena.take([NDC, NHALO + NCTX], F32)
    XE_BYTES = arena.off
    dxe = [[Dep(), Dep()] for _ in range(NDC)]
    for dc in range(NDC):
        m.dma("sync", xe[:, dc, 0:NHALO], xin[:, dc, NOWN:NOWN + NHALO], writes=[dxe[dc][0]])
        m.dma("sync", xe[:, dc, NHALO:], xin[:, dc, NOWN + NHALO:], writes=[dxe[dc][1]])

    m.op("scalar", lambda e: e.activation(out=cvb[:], in_=cvs[:], func=AF.Silu), reads=[d_cv], writes=[d_cv])

    class NS:
        pass

    W = NS()

    def carve_norm():
        W.sq = arena.rot(1, [NDC, 512], BF16)
        W.rstd = arena.rot(2, [512], F32)
        W.tmpf = arena.rot(2, [512], F32)

    def carve_ffn():
        arena.reset(XE_BYTES)
        W.hT = arena.take([NDC, 1040], BF16)
        W.d_hT = [Dep(), Dep(), Dep()]
        W.actT = arena.take([NF, 1040], BF16)
        W.d_act = [[Dep(), Dep(), Dep()] for _ in range(NF)]
        carve_norm()
        W.wi_t = arena.rot(3, [NDC, 256], BF16)
        W.wo_t = arena.rot(2, [NF, 128], BF16)
        W.wm_t = arena.rot(2, [2, NDC, 128], BF16)

    def adaln(l):
        ps, dps = banks.get()
        for c2 in range(36):
            wt, dwt = W.wm_t.get()
            m.dma("gpsimd", wt, wmod[l, c2 * 2:(c2 + 1) * 2].rearrange("c p (k f) -> p c k f", f=128), writes=[dwt])

            def mm(e, wt=wt, c2=c2, ps=ps):
                r = None
                for c in range(2):
                    cc = c2 * 2 + c
                    for k in range(NDC):
                        r = e.matmul(ps[:, cc * 2:cc * 2 + 2], wt[:, c, k, :], cvb[:, k, :],
                                     start=(k == 0), stop=(k == NDC - 1))
                return r

            m.op("tensor", mm, reads=[dwt, d_cv], writes=[dps])
        mv = modv[l]
        for col in range(2):
            m.op("vector", lambda e, col=col: e.tensor_tensor(
                out=mv[:, :, col], in0=ps[:, 0:144].rearrange("p (c t) -> p c t", t=2)[:, :, col],
                in1=bmodT[:, l, :], op=ALU.add), reads=[dps, d_small], writes=[d_modv[l]])
        for sub in range(3):
            for col in range(2):
                m.op("vector", lambda e, sub=sub, col=col: e.scalar_tensor_tensor(
                    out=gsv[l][:, sub, :, col], in0=mv[:, (3 * sub + 1) * 8:(3 * sub + 2) * 8, col], scalar=1.0,
                    in1=ngT[:, l, sub, :], op0=ALU.add, op1=ALU.mult),
                    reads=[d_modv[l], d_small], writes=[d_gs[l]])
                m.op("vector", lambda e, sub=sub, col=col: e.tensor_scalar(
                    out=hgv[l][:, sub, :, col], in0=mv[:, (3 * sub + 2) * 8:(3 * sub + 3) * 8, col],
                    scalar1=(1.0 if sub == 1 else 0.5), scalar2=None, op0=ALU.mult),
                    reads=[d_modv[l]], writes=[d_gs[l]])

    def rms_rstd(xsrc, dxs, n, nparts=NDC, inv_n=1.0 / D):
        sqt, dsq = W.sq.get()
        for dc in range(nparts):
            m.op("vector", lambda e, dc=dc: e.tensor_tensor(out=sqt[:, dc, 0:n], in0=xsrc(dc), in1=xsrc(dc),
                                                            op=ALU.mult), reads=[dxs[dc]], writes=[dsq])
        ps, dps = banks.get()

        def mm(e):
            r = None
            for dc in range(nparts):
                r = e.matmul(ps[:, 0:n], ones[:], sqt[:, dc, 0:n], start=(dc == 0), stop=(dc == nparts - 1))
            return r

        m.op("tensor", mm, reads=[dsq, d_ones], writes=[dps])
        rt, drt = W.rstd.get()
        m.op("scalar", lambda e: e.activation(out=rt[:, 0:n], in_=ps[:, 0:n], func=AF.Ln, scale=inv_n,
                                              bias=epsT[:, 0:1]), reads=[dps, d_eps], writes=[drt])
        m.op("scalar", lambda e: e.activation(out=rt[:, 0:n], in_=rt[:, 0:n], func=AF.Exp, scale=-0.5),
             reads=[drt], writes=[drt])
        return rt, drt

    def modulate(l, sub, col, xsrc, dxs, n, hdst, dh):
        rt, drt = rms_rstd(xsrc, dxs, n)
        for dc in range(NDC):
            tf, dtf = W.tmpf.get()
            m.op("vector", lambda e, dc=dc, tf=tf: e.tensor_tensor(out=tf[:, 0:n], in0=xsrc(dc), in1=rt[:, 0:n],
                                                                   op=ALU.mult),
                 reads=[dxs[dc], drt], writes=[dtf])
            m.op("vector", lambda e, dc=dc, tf=tf: e.tensor_scalar(
                out=hdst(dc), in0=tf[:, 0:n], scalar1=gsv[l][:, sub, dc, col:col + 1],
                scalar2=modv[l][:, 3 * sub * 8 + dc, col:col + 1], op0=ALU.mult, op1=ALU.add),
                reads=[dtf, d_gs[l], d_modv[l]], writes=[dh])

    def ffn_supertile(l, which, sub, segs):
        hT, actT = W.hT, W.actT
        offs = [0, 512, 1024]
        for j, (xsrc, dxs, n, col) in enumerate(segs):
            modulate(l, sub, col, xsrc, dxs, n,
                     lambda dc, j=j, n=n: hT[:, dc, offs[j]:offs[j] + n], W.d_hT[j])
        ns = len(segs)
        for f in range(NF):
            wt, dwt = W.wi_t.get()
            m.dma("gpsimd", wt, wi_d[l, which, f].rearrange("p (k c) -> p k c", c=256), writes=[dwt])
            pa = []
            for half in range(2):
                for j, (xsrc, dxs, n, col) in enumerate(segs):
                    ps, dps = banks.get()

                    def mm(e, wt=wt, half=half, j=j, n=n, ps=ps):
                        r = None
                        for k in range(NDC):
                            r = e.matmul(ps[:, 0:n], wt[:, k, half * 128:(half + 1) * 128],
                                         hT[:, k, offs[j]:offs[j] + n], start=(k == 0), stop=(k == NDC - 1))
                        return r

                    m.op("tensor", mm, reads=[dwt, W.d_hT[j]], writes=[dps])
                    pa.append((ps, dps))
                if half == 0 and ns == 3:
                    pass
            for j, (xsrc, dxs, n, col) in enumerate(segs):
                psa, dpsa = pa[j]
                psu, dpsu = pa[ns + j]
                tf, dtf = W.tmpf.get()
                m.op("scalar", lambda e, tf=tf, psa=psa, n=n: e.activation(out=tf[:, 0:n], in_=psa[:, 0:n],
                                                                          func=AF.Silu),
                     reads=[dpsa], writes=[dtf])
                m.op("vector", lambda e, tf=tf, psu=psu, n=n, f=f, j=j: e.tensor_tensor(
                    out=actT[:, f, offs[j]:offs[j] + n], in0=tf[:, 0:n], in1=psu[:, 0:n], op=ALU.mult),
                    reads=[dtf, dpsu], writes=[W.d_act[f][j]])
        for dc in range(NDC):
            wt, dwt = W.wo_t.get()
            m.dma("gpsimd", wt, wo_d[l, which, dc].rearrange("p (k c) -> p k c", c=128), writes=[dwt])
            for j, (xsrc, dxs, n, col) in enumerate(segs):
                ps, dps = banks.get()

                def mm(e, wt=wt, j=j, n=n, ps=ps):
                    r = None
                    for f in range(NF):
                        r = e.matmul(ps[:, 0:n], wt[:, f, :], actT[:, f, offs[j]:offs[j] + n],
                                     start=(f == 0), stop=(f == NF - 1))
                    return r

                m.op("tensor", mm, reads=[dwt] + [W.d_act[f][j] for f in range(NF)], writes=[dps])
                m.op("vector", lambda e, dc=dc, ps=ps, n=n, xsrc=xsrc, col=col: e.scalar_tensor_tensor(
                    out=xsrc(dc), in0=ps[:, 0:n], scalar=hgv[l][:, sub, dc, col:col + 1], in1=xsrc(dc),
                    op0=ALU.mult, op1=ALU.add), reads=[dps, d_gs[l]], writes=[dxs[dc]])

    def own_seg(q):
        return (lambda dc, q=q: xT[:, dc, q * 512:(q + 1) * 512], [dx[dc][q] for dc in range(NDC)], 512, 0)

    halo_seg = (lambda dc: xe[:, dc, 0:NHALO], [dxe[dc][0] for dc in range(NDC)], NHALO, 0)
    ctx_seg = (lambda dc: xe[:, dc, NHALO:NHALO + NCTX], [dxe[dc][1] for dc in range(NDC)], NCTX, 1)

    def ffn_all(l, which, sub, extra=False):
        ffn_supertile(l, which, sub, [own_seg(0), own_seg(1)])
        ffn_supertile(l, which, sub, [own_seg(2), own_seg(3)])
        if extra:
            ffn_supertile(l, which, sub, [ctx_seg, halo_seg])

    def evac(eng_i, out, in_, reads, writes, scale=None):
        if eng_i % 2 == 0:
            if scale is None:
                m.op("vector", lambda e: e.tensor_copy(out=out, in_=in_), reads=reads, writes=writes)
            else:
                m.op("vector", lambda e: e.tensor_scalar(out=out, in0=in_, scalar1=scale, scalar2=None,
                                                         op0=ALU.mult), reads=reads, writes=writes)
        else:
            if scale is None:
                m.op("scalar", lambda e: e.activation(out=out, in_=in_, func=AF.Identity), reads=reads, writes=writes)
            else:
                m.op("scalar", lambda e: e.activation(out=out, in_=in_, func=AF.Identity, scale=scale),
                     reads=reads, writes=writes)

    def mixer0():
        arena.reset(XE_BYTES)
        hTg = arena.take([NDC, 512], BF16)
        d_hTg = Dep()
        carve_norm()
        wp_t = arena.rot(3, [NDC, 128], BF16)
        wva_t = arena.take([NDC, 128], BF16)
        wvb_t = arena.take([NDC, 512], BF16)
        wkbt_t = arena.take([NDC, 256], BF16)
        d_wtm = Dep()
        cs_t = arena.rot(1, [2, 512], F32)
        stg = arena.rot(4, [512], F32)
        m.dma("gpsimd", wva_t, wva_d.rearrange("p (k c) -> p k c", c=128), writes=[d_wtm])
        m.dma("gpsimd", wvb_t, wvb_d.rearrange("p (k c) -> p k c", c=512), writes=[d_wtm])
        m.dma("gpsimd", wkbt_t, wkbt_d.rearrange("p (k c) -> p k c", c=256), writes=[d_wtm])
        d_scr = {k: Dep() for k in ("QA", "KA", "VA", "QB", "KB", "KBT", "VB", "ZG", "SRB", "OX", "OB", "OA")}
        evi = [0]

        def fm_chunk(ci, n):
            wt, dwt = wp_t.get()
            m.dma("gpsimd", wt, wp_d[ci].rearrange("p (k c) -> p k c", c=128), writes=[dwt])
            ps, dps = banks.get()

            def mm(e):
                r = None
                for k in range(NDC):
                    r = e.matmul(ps[:, 0:n], wt[:, k, :], hTg[:, k, 0:n], start=(k == 0), stop=(k == NDC - 1))
                return r

            m.op("tensor", mm, reads=[dwt, d_hTg], writes=[dps])
            return ps, dps

        def store(dst, src_ap, dsrc, key):
            m.dma("sync", dst, src_ap, reads=[dsrc], writes=[d_scr[key]])

        def rope_pair(ci, cip, n, cst, dcs, dst, key):
            p1, d1 = fm_chunk(ci, n)
            p2, d2 = fm_chunk(cip, n)
            s1, ds1 = stg.get()
            s2, ds2 = stg.get()
            m.op("vector", lambda e: e.tensor_tensor(out=s1[:, 0:n], in0=p1[:, 0:n], in1=cst[:, 0, 0:n], op=ALU.mult),
                 reads=[d1, dcs], writes=[ds1])
            m.op("vector", lambda e: e.tensor_tensor(out=s2[:, 0:n], in0=p2[:, 0:n], in1=cst[:, 1, 0:n], op=ALU.mult),
                 reads=[d2, dcs], writes=[ds2])
            s3, ds3 = stg.get()
            s3b = s3.bitcast(BF16)
            m.op("vector", lambda e: e.tensor_tensor(out=s3b[:, 0:n], in0=s1[:, 0:n], in1=s2[:, 0:n], op=ALU.add),
                 reads=[ds1, ds2], writes=[ds3])
            store(dst, s3b[:, 0:n], ds3, key)

        def plain(ci, n, dst, key, scale=None, nrow=128, silu=False, f32=False):
            p1, d1 = fm_chunk(ci, n)
            s1, ds1 = stg.get()
            sv = s1 if f32 else s1.bitcast(BF16)
            if silu:
                m.op("scalar", lambda e: e.activation(out=sv[:, 0:n], in_=p1[:, 0:n], func=AF.Silu),
                     reads=[d1], writes=[ds1])
            else:
                evac(evi[0], sv[0:nrow, 0:n], p1[0:nrow, 0:n], [d1], [ds1], scale)
                evi[0] += 1
            store(dst, sv[0:nrow, 0:n], ds1, key)

        def tok_major(wt, ncol, blk_col0, nblk, dst_fn, key):
            for b in range(nblk):
                ps, dps = banks.get()

                def mm(e, b=b, ps=ps):
                    r = None
                    for k in range(NDC):
                        r = e.matmul(ps[:, 0:ncol], hTg[:, k, b * 128:(b + 1) * 128], wt[:, k, :],
                                     start=(k == 0), stop=(k == NDC - 1))
                    return r

                m.op("tensor", mm, reads=[d_wtm, d_hTg], writes=[dps])
                s1, ds1 = stg.get()
                sv = s1.bitcast(BF16)
                evac(evi[0], sv[:, 0:ncol], ps[:, 0:ncol], [dps], [ds1])
                evi[0] += 1
                store(dst_fn(b), sv[:, 0:ncol], ds1, key)

        groups = [("own", q) for q in range(4)] + [("halo", 0), ("ctx", 0)]
        for kind, q in groups:
            if kind == "own":
                xsrc, dxs, n, col = own_seg(q)
                c0 = q * 512
            elif kind == "halo":
                xsrc, dxs, n, col = halo_seg
                c0 = NOWN
            else:
                xsrc, dxs, n, col = ctx_seg
                c0 = NOWN + NHALO
            modulate(0, 1, col, xsrc, dxs, n, lambda dc, n=n: hTg[:, dc, 0:n], d_hTg)
            if kind != "ctx":
                cst, dcs = cs_t.get()
                m.dma("sync", cst[:, 0, 0:n], cos_d[:, c0:c0 + n], writes=[dcs])
                m.dma("sync", cst[:, 1, 0:n], sin_d[:, c0:c0 + n], writes=[dcs])
            if kind == "own":
                for c in range(4):
                    rope_pair(C_QA + c, C_QAP + c, n, cst, dcs, QA[c, :, c0:c0 + n], "QA")
            if kind != "ctx":
                for g in range(2):
                    rope_pair(C_KA + g, C_KAP + g, n, cst, dcs, KA[g, :, c0:c0 + n], "KA")
            else:
                for g in range(2):
                    plain(C_KA + g, n, KA[g, :, c0:c0 + n], "KA")
            blk0 = {"own": q * 4, "halo": 16, "ctx": 17}[kind]
            tok_major(wva_t, 128, 0, n // 128, lambda b, blk0=blk0: VA[blk0 + b], "VA")
            if kind in ("own", "ctx"):
                cb = c0 if kind == "own" else NOWN
                bb = q * 4 if kind == "own" else 16
                for c in range(2):
                    plain(C_KB + c, n, KB[c, :, cb:cb + n], "KB")
                plain(C_ZGX, n, ZG[0, :, cb:cb + n], "ZG")
                tok_major(wkbt_t, 256, 0, n // 128, lambda b, bb=bb: KBT[bb + b], "KBT")
                tok_major(wvb_t, 512, 0, n // 128, lambda b, bb=bb: VB[bb + b], "VB")
            if kind == "own":
                plain(C_ZGY, n, ZG[1, :, c0:c0 + n], "ZG")
                for c in range(2):
                    plain(C_QB + c, n, QB[c, :, c0:c0 + n], "QB", scale=0.125)
                for c in range(4):
                    plain(C_RB + c, n, SRB[c, :, c0:c0 + n], "SRB", silu=True, f32=True)

        m.barrier(dummy[:])
        if stop == "proj":
            return

        arena.reset(0)
        wa2_t = arena.take([2, 256], BF16)
        ba_t = arena.take([2, 256], F32)
        trib_t = arena.take([2, 128], BF16)
        Lhl_t = arena.rot(2, [2, 256], BF16)
        mP_t = arena.take([512], BF16)
        mN_t = arena.take([512], BF16)
        sinkE = arena.take([8, 128], F32)
        sinkr = arena.take([8], F32)
        d_c2 = Dep()
        m.op("vector", lambda e: e.memset(wa2_t, 0.0), writes=[d_c2])
        m.dma("gpsimd", wa2_t[0:16], wa2_d, writes=[d_c2])
        m.dma("sync", ba_t, ba_d, writes=[d_c2])
        m.dma("gpsimd", trib_t, tri_d, writes=[d_c2])
        m.dma("gpsimd", mP_t, mP_d, writes=[d_c2])
        m.dma("gpsimd", mN_t, mN_d, writes=[d_c2])
        m.dma("sync", sinkr, sink_d, writes=[d_c2])
        m.op("scalar", lambda e: e.activation(out=sinkr, in_=sinkr, func=AF.Exp), reads=[d_c2], writes=[d_c2])
        m.op("vector", lambda e: e.memset(sinkE, 0.0), writes=[d_c2])
        for h in range(8):
            m.op("vector", lambda e, h=h: e.tensor_scalar(out=sinkE[:, h, :], in0=sinkE[:, h, :],
                                                          scalar1=sinkr[:, h:h + 1], scalar2=None, op0=ALU.add),
                 reads=[d_c2], writes=[d_c2])
        Sst = arena.take([2, 128], F32)
        Sbf = arena.take([2, 128], BF16)
        d_S = [Dep(), Dep()]
        kz = arena.rot(2, [2, 2, 128], BF16)
        for kzt, dkz in kz.items:
            m.op("vector", lambda e, kzt=kzt: e.memset(kzt, 0.0), writes=[dkz])
        zg_t = arena.rot(2, [128], BF16)
        for zt_, dzt_ in zg_t.items:
            m.op("vector", lambda e, zt_=zt_: e.memset(zt_, 0.0), writes=[dzt_])
        qb_t = arena.rot(2, [2, 128], BF16)
        kb_t = arena.rot(2, [2, 128], BF16)
        kbt_t = arena.rot(2, [256], BF16)
        vb_t = arena.rot(2, [512], BF16)
        L_t = arena.rot(2, [256], F32)
        eq_t = arena.rot(2, [2, 128], F32)
        ek_t = arena.rot(2, [2, 128], F32)
        ekt_t = arena.rot(2, [256], F32)
        qt_t = arena.rot(2, [2, 2, 128], BF16)
        for qz_, dqz_ in qt_t.items:
            m.op("vector", lambda e, qz_=qz_: e.memset(qz_, 0.0), writes=[dqz_])
        kt_t = arena.rot(2, [2, 128], BF16)
        attm_t = arena.rot(2, [4, 128], BF16)
        och_t = arena.rot(2, [512], F32)
        ox_t = arena.rot(2, [512], F32)
        srb_t = arena.rot(2, [4, 128], F32)
        sqo_t = arena.rot(2, [512], BF16)
        ro_t = arena.rot(2, [512], F32)
        ob_t = arena.rot(2, [4, 128], BF16)
        g8_t = arena.take([ncc, 256], F32)
        d_g8 = Dep()
        cka_t = arena.take([2, NCTX], BF16)
        cva_t = arena.take([2, 128], BF16)
        d_catt = Dep()
        qa_t = arena.rot(2, [4, 128], BF16)
        for qz_, dqz_ in qa_t.items:
            m.op("vector", lambda e, qz_=qz_: e.memset(qz_, 0.0), writes=[dqz_])
        ka_t = arena.rot(2, [384], BF16)
        va_t = arena.rot(2, [3, 128], BF16)
        P_t = arena.rot(4, [512], BF16)
        den_t = arena.rot(2, [512], F32)
        oa_t = arena.rot(2, [4, 128], BF16)
        wouta_t = arena.take([4, D], BF16)
        woutb_t = arena.take([4, D], BF16)
        d_wout = Dep()
        oag_t = arena.rot(2, [4, 512], BF16)
        obg_t = arena.rot(2, [4, 512], BF16)
        m.dma("gpsimd", wouta_t, wouta_d, writes=[d_wout])
        m.dma("gpsimd", woutb_t, woutb_d, writes=[d_wout])

        def gla_chunk(dirn, colbase, blk, with_out, ochunk_idx):
            zt, dzt = zg_t.get()
            m.dma("sync", zt, ZG[dirn, :, colbase:colbase + 128], reads=[d_scr["ZG"]], writes=[dzt])
            kbt, dkbt = kbt_t.get()
            m.dma("sync", kbt, KBT[blk], reads=[d_scr["KBT"]], writes=[dkbt])
            vbt, dvbt = vb_t.get()
            m.dma("sync", vbt, VB[blk], reads=[d_scr["VB"]], writes=[dvbt])
            if with_out:
                qbt, dqbt = qb_t.get()
                m.dma("sync", qbt, QB[:, :, colbase:colbase + 128].rearrange("c p t -> p c t"),
                      reads=[d_scr["QB"]], writes=[dqbt])
                kb, dkb = kb_t.get()
                m.dma("sync", kb, KB[:, :, colbase:colbase + 128].rearrange("c p t -> p c t"),
                      reads=[d_scr["KB"]], writes=[dkb])
            ps, dps = banks.get()
            m.op("tensor", lambda e: e.matmul(ps[:, 0:256], zt, wa2_t[:, dirn, :], start=True, stop=True),
                 reads=[dzt, d_c2], writes=[dps])
            Lt, dLt = L_t.get()
            m.op("vector", lambda e: e.tensor_tensor(out=Lt, in0=ps[:, 0:256], in1=ba_t[:, dirn, :], op=ALU.add),
                 reads=[dps, d_c2], writes=[dLt])
            m.op("scalar", lambda e: e.activation(out=Lt, in_=Lt, func=AF.Exp, scale=-1.0), reads=[dLt], writes=[dLt])
            m.op("scalar", lambda e: e.activation(out=Lt, in_=Lt, func=AF.Ln, bias=1.0), reads=[dLt], writes=[dLt])
            Lh, dLh = Lhl_t.get()
            m.op("vector", lambda e: e.tensor_copy(out=Lh[:, 0, :], in_=Lt), reads=[dLt], writes=[dLh])
            m.op("vector", lambda e: e.tensor_tensor(out=Lt, in0=Lt, in1=Lh[:, 0, :], op=ALU.subtract),
                 reads=[dLt, dLh], writes=[dLt])
            m.op("vector", lambda e: e.tensor_copy(out=Lh[:, 1, :], in_=Lt), reads=[dLt], writes=[dLh])
            psG, dpsG = banks.get()

            def mmG(e):
                r = None
                for c in range(2):
                    for hl in range(2):
                        r = e.matmul(psG[:, c * 128:(c + 1) * 128], Lh[:, hl, c * 128:(c + 1) * 128],
                                     trib_t[:, dirn, :], start=(hl == 0), stop=(hl == 1))
                for hl in range(2):
                    r = e.matmul(psG[:, 256:512], trib_t[:, dirn, :], Lh[:, hl, :], start=(hl == 0), stop=(hl == 1))
                return r

            m.op("tensor", mmG, reads=[dLh, d_c2], writes=[dpsG])
            eq, deq = eq_t.get()
            ek, dek = ek_t.get()
            ekt, dekt = ekt_t.get()
            m.op("scalar", lambda e: e.activation(out=eq, in_=psG[:, 0:256].rearrange("p (c t) -> p c t", t=128),
                                                  func=AF.Exp, scale=-1.0 / 16), reads=[dpsG], writes=[deq])
            if with_out:
                m.op("scalar", lambda e: e.activation(out=ek, in_=psG[:, 0:256].rearrange("p (c t) -> p c t", t=128),
                                                      func=AF.Exp, scale=1.0 / 16), reads=[dpsG], writes=[dek])
            m.op("scalar", lambda e: e.activation(out=ekt, in_=psG[:, 256:512], func=AF.Exp, scale=1.0 / 16),
                 reads=[dpsG], writes=[dekt])
            lastc = 127 if dirn == 0 else 0
            kzt, dkz = kz.get()
            for c in range(2):
                for r in range(2):
                    h = 2 * c + r
                    m.op("vector", lambda e, c=c, r=r, h=h: e.tensor_tensor(
                        out=kzt[:, c, r, r * 64:(r + 1) * 64], in0=kbt[:, h * 64:(h + 1) * 64],
                        in1=ekt[:, h * 64:(h + 1) * 64], op=ALU.mult), reads=[dkbt, dekt], writes=[dkz])
            if with_out:
                qt, dqt = qt_t.get()
                kt, dkt = kt_t.get()
                for r in range(2):
                    m.op("vector", lambda e, r=r: e.tensor_tensor(
                        out=qt[r * 64:(r + 1) * 64, :, r, :], in0=qbt[r * 64:(r + 1) * 64], in1=eq[r * 64:(r + 1) * 64],
                        op=ALU.mult), reads=[dqbt, deq], writes=[dqt])
                m.op("vector", lambda e: e.tensor_tensor(out=kt, in0=kb, in1=ek, op=ALU.mult),
                     reads=[dkb, dek], writes=[dkt])
                psA, dpsA = banks.get()

                def mmA(e):
                    r_ = None
                    for c in range(2):
                        for r in range(2):
                            h = 2 * c + r
                            r_ = e.matmul(psA[:, h * 128:(h + 1) * 128], kt[:, c, :], qt[:, c, r, :],
                                          start=True, stop=True)
                    return r_

                m.op("tensor", mmA, reads=[dqt, dkt], writes=[dpsA])
                am, dam = attm_t.get()
                msk = mN_t if dirn == 0 else mP_t
                m.op("vector", lambda e: e.tensor_tensor(out=am, in0=psA.rearrange("p (h t) -> p h t", t=128),
                                                         in1=msk.rearrange("p (h t) -> p h t", t=128), op=ALU.mult),
                     reads=[dpsA, d_c2], writes=[dam])
                psO, dpsO = banks.get()

                def mmO(e):
                    r_ = None
                    for c in range(2):
                        for r in range(2):
                            h = 2 * c + r
                            e.matmul(psO[:, h * 128:(h + 1) * 128], vbt[:, h * 128:(h + 1) * 128], am[:, h, :],
                                     start=True, stop=False)
                            r_ = e.matmul(psO[:, h * 128:(h + 1) * 128], Sbf[:, c, :], qt[:, c, r, :],
                                          start=False, stop=True)
                    return r_

                m.op("tensor", mmO, reads=[dvbt, dam, dqt, d_S[0], d_S[1]], writes=[dpsO])
            psS, dpsS = banks.get()

            def mmS(e):
                r_ = None
                for c in range(2):
                    e.matmul(psS[:, c * 128:(c + 1) * 128], kzt[:, c, 0, :], vbt[:, (2 * c) * 128:(2 * c + 1) * 128],
                             start=True, stop=False)
                    r_ = e.matmul(psS[:, c * 128:(c + 1) * 128], kzt[:, c, 1, :],
                                  vbt[:, (2 * c + 1) * 128:(2 * c + 2) * 128], start=False, stop=True)
                return r_

            m.op("tensor", mmS, reads=[dkz, dvbt], writes=[dpsS])
            for c in range(2):
                m.op("vector", lambda e, c=c: e.tensor_tensor(out=Sst[:, c, :], in0=Sst[:, c, :],
                                                              in1=psS[:, c * 128:(c + 1) * 128], op=ALU.add),
                     reads=[dpsS], writes=[d_S[c]])
                m.op("vector", lambda e, c=c: e.tensor_scalar(out=Sst[:, c, :], in0=Sst[:, c, :],
                                                              scalar1=eq[:, c, lastc:lastc + 1], scalar2=None,
                                                              op0=ALU.mult), reads=[deq, d_S[c]], writes=[d_S[c]])
                m.op("vector", lambda e, c=c: e.tensor_copy(out=Sbf[:, c, :], in_=Sst[:, c, :]),
                     reads=[d_S[c]], writes=[d_S[c]])
            if not with_out:
                return
            if dirn == 0:
                oc, doc = och_t.get()
                m.op("scalar", lambda e: e.activation(out=oc, in_=psO, func=AF.Identity), reads=[dpsO], writes=[doc])
                m.dma("sync", OX[ochunk_idx], oc, reads=[doc], writes=[d_scr["OX"]])
                return
            oxt, doxt = ox_t.get()
            m.dma("sync", oxt, OX[ochunk_idx], reads=[d_scr["OX"]], writes=[doxt])
            srt, dsrt = srb_t.get()
            m.dma("sync", srt, SRB[:, :, colbase:colbase + 128].rearrange("c p t -> p c t"),
                  reads=[d_scr["SRB"]], writes=[dsrt])
            oc, doc = och_t.get()
            m.op("vector", lambda e: e.tensor_tensor(out=oc, in0=psO, in1=oxt, op=ALU.add),
                 reads=[dpsO, doxt], writes=[doc])
            sqo, dsqo = sqo_t.get()
            m.op("vector", lambda e: e.tensor_tensor(out=sqo, in0=oc, in1=oc, op=ALU.mult), reads=[doc], writes=[dsqo])
            psN, dpsN = banks.get()
            m.op("tensor", lambda e: e.matmul(psN, ones[:], sqo, start=True, stop=True),
                 reads=[dsqo, d_ones], writes=[dpsN])
            ro, dro = ro_t.get()
            m.op("scalar", lambda e: e.activation(out=ro, in_=psN, func=AF.Ln, scale=1.0 / 128, bias=epsT[:, 0:1]),
                 reads=[dpsN, d_eps], writes=[dro])
            m.op("scalar", lambda e: e.activation(out=ro, in_=ro, func=AF.Exp, scale=-0.5), reads=[dro], writes=[dro])
            m.op("vector", lambda e: e.tensor_tensor(out=oc, in0=oc, in1=ro, op=ALU.mult), reads=[doc, dro], writes=[doc])
            obt, dobt = ob_t.get()
            m.op("vector", lambda e: e.scalar_tensor_tensor(
                out=obt, in0=oc.rearrange("p (h t) -> p h t", t=128), scalar=glagT[:, 0:1], in1=srt,
                op0=ALU.mult, op1=ALU.mult), reads=[doc, dsrt, d_small], writes=[dobt])
            m.dma("sync", OB[:, :, colbase:colbase + 128].rearrange("c p t -> p c t"), obt,
                  reads=[dobt], writes=[d_scr["OB"]])

        GLA = "nogla" not in flags
        ATT = "noattn" not in flags
        for c in range(2):
            m.op("vector", lambda e, c=c: e.memset(Sst[:, c, :], 0.0), writes=[d_S[c]])
            m.op("vector", lambda e, c=c: e.memset(Sbf[:, c, :], 0.0), writes=[d_S[c]])
        for j in range(2 if GLA else 0):
            gla_chunk(0, NOWN + j * 128, 16 + j, False, None)
        for n in range(16 if GLA else 0):
            gla_chunk(0, n * 128, n, True, n)
        d_cc1 = Dep()
        m.dma("sync", cc1_src, Sst.rearrange("p c v -> p (c v)"), reads=[d_S[0], d_S[1]], writes=[d_cc1])
        d_cc1o = Dep()
        m.op("gpsimd", lambda e: e.collective_compute("AllGather", ALU.bypass, replica_groups=[list(range(ncc))],
                                                      ins=[cc1_src.opt()], outs=[cc1_dst.opt()]),
             reads=[d_cc1], writes=[d_cc1o])

        m.dma("sync", cka_t, KA[:, :, NOWN + NHALO:].rearrange("g p t -> p g t"), reads=[d_scr["KA"]], writes=[d_catt])
        m.dma("sync", cva_t, VA[17:19].rearrange("b p c -> p b c"), reads=[d_scr["VA"]], writes=[d_catt])
        for i in range(16 if ATT else 0):
            lo = max(i - 1, 0)
            nkb = (i + 2) - lo
            vat, dvat = va_t.get()
            m.dma("sync", vat[:, 0:nkb, :], VA[lo:lo + nkb].rearrange("b p c -> p b c"),
                  reads=[d_scr["VA"]], writes=[dvat])
            for g in range(2):
                qat, dqat = qa_t.get()
                for r in range(4):
                    off = (r % 2) * 64
                    m.dma("sync", qat[off:off + 64, r, :], QA[2 * g + r // 2, off:off + 64, i * 128:(i + 1) * 128],
                          reads=[d_scr["QA"]], writes=[dqat])
                kat, dkat = ka_t.get()
                m.dma("sync", kat[:, 0:nkb * 128], KA[g, :, lo * 128:(lo + nkb) * 128],
                      reads=[d_scr["KA"]], writes=[dkat])
                psO, dpsO = banks.get()
                psD, dpsD = banks.get()
                kblocks = [("band", b) for b in range(nkb)] + [("ctx", 0), ("ctx", 1)]
                nk = len(kblocks)
                for bi, (kind, b) in enumerate(kblocks):
                    psS, dpsS = banks.get()
                    if kind == "band":
                        ksrc = lambda b=b, kat=kat: kat[:, b * 128:(b + 1) * 128]
                        kdeps = [dkat]
                        vsrc = vat[:, b, :]
                        vdeps = [dvat]
                        absb = lo + b
                        mask = mP_t if absb == i - 1 else (mN_t if absb == i + 1 else None)
                    else:
                        ksrc = lambda b=b, g=g: cka_t[:, g, b * 128:(b + 1) * 128]
                        kdeps = [d_catt]
                        vsrc = cva_t[:, b, :]
                        vdeps = [d_catt]
                        mask = None

                    def mmS(e, ksrc=ksrc, psS=psS, qat=qat):
                        r_ = None
                        for r in range(4):
                            r_ = e.matmul(psS[:, r * 128:(r + 1) * 128], ksrc(), qat[:, r, :],
                                          start=True, stop=True)
                        return r_

                    m.op("tensor", mmS, reads=kdeps + [dqat], writes=[dpsS])
                    Pt, dPt = P_t.get()
                    m.op("scalar", lambda e, Pt=Pt, psS=psS: e.activation(out=Pt, in_=psS, func=AF.Exp, scale=0.125),
                         reads=[dpsS], writes=[dPt])
                    if mask is not None:
                        m.op("vector", lambda e, Pt=Pt, mask=mask: e.tensor_tensor(out=Pt, in0=Pt, in1=mask,
                                                                                   op=ALU.mult),
                             reads=[dPt, d_c2], writes=[dPt])

                    def mmO(e, Pt=Pt, vsrc=vsrc, bi=bi, psO=psO, psD=psD, nk=nk):
                        e.matmul(psO, vsrc, Pt, start=(bi == 0), stop=(bi == nk - 1))
                        return e.matmul(psD, ones[:], Pt, start=(bi == 0), stop=(bi == nk - 1))

                    m.op("tensor", mmO, reads=vdeps + [dPt, d_ones], writes=[dpsO, dpsD])
                dn, ddn = den_t.get()
                m.op("vector", lambda e, dn=dn, psD=psD, g=g: e.tensor_tensor(
                    out=dn, in0=psD, in1=sinkE[:, 4 * g:4 * g + 4, :].rearrange("p h t -> p (h t)"),
                    op=ALU.add), reads=[dpsD, d_c2], writes=[ddn])
                m.op("vector", lambda e, dn=dn: e.reciprocal(out=dn, in_=dn), reads=[ddn], writes=[ddn])
                oat, doat = oa_t.get()
                m.op("vector", lambda e, dn=dn, psO=psO, oat=oat: e.tensor_tensor(
                    out=oat.rearrange("p h t -> p (h t)"), in0=psO, in1=dn, op=ALU.mult),
                    reads=[dpsO, ddn], writes=[doat])
                for r in range(4):
                    h = 4 * g + r
                    m.dma("sync", OA[h // 2, (h % 2) * 64:(h % 2) * 64 + 64, i * 128:(i + 1) * 128],
                          oat[g * 64:(g + 1) * 64, r, :], reads=[doat], writes=[d_scr["OA"]])

        m.dma("sync", g8_t, cc1_dst.rearrange("(r p) f -> p r f", p=128), reads=[d_cc1o], writes=[d_g8])
        Sflat = Sst.rearrange("p c v -> p (c v)")
        m.op("vector", lambda e: e.tensor_scalar(out=Sflat, in0=g8_t[:, 0, :], scalar1=pselT[:, 0:1], scalar2=None,
                                                 op0=ALU.mult), reads=[d_g8, d_small, d_cc1], writes=[d_S[0], d_S[1]])
        for r in range(1, ncc):
            m.op("vector", lambda e, r=r: e.scalar_tensor_tensor(out=Sflat, in0=g8_t[:, r, :],
                                                                 scalar=pselT[:, r:r + 1], in1=Sflat,
                                                                 op0=ALU.mult, op1=ALU.add),
                 reads=[d_g8, d_small], writes=[d_S[0], d_S[1]])
        for c in range(2):
            m.op("vector", lambda e, c=c: e.tensor_copy(out=Sbf[:, c, :], in_=Sst[:, c, :]),
                 reads=[d_S[c]], writes=[d_S[c]])
        for n in (range(15, -1, -1) if GLA else ()):
            gla_chunk(1, n * 128, n, True, n)

        for q in range(4):
            oag, doag = oag_t.get()
            m.dma("sync", oag, OA[:, :, q * 512:(q + 1) * 512].rearrange("c p t -> p c t"),
                  reads=[d_scr["OA"]], writes=[doag])
            obg, dobg = obg_t.get()
            m.dma("sync", obg, OB[:, :, q * 512:(q + 1) * 512].rearrange("c p t -> p c t"),
                  reads=[d_scr["OB"]], writes=[dobg])
            for dc in range(NDC):
                ps, dps = banks.get()

                def mm(e, dc=dc, ps=ps, oag=oag, obg=obg):
                    ops_ = []
                    if ATT:
                        ops_ += [(wouta_t[:, c, dc * 128:(dc + 1) * 128], oag[:, c, :]) for c in range(4)]
                    if GLA:
                        ops_ += [(woutb_t[:, c, dc * 128:(dc + 1) * 128], obg[:, c, :]) for c in range(4)]
                    r_ = None
                    for i_, (l_, r__) in enumerate(ops_):
                        r_ = e.matmul(ps, l_, r__, start=(i_ == 0), stop=(i_ == len(ops_) - 1))
                    return r_

                m.op("tensor", mm, reads=[d_wout, doag, dobg], writes=[dps])
                m.op("vector", lambda e, dc=dc, ps=ps, q=q: e.scalar_tensor_tensor(
                    out=xT[:, dc, q * 512:(q + 1) * 512], in0=ps, scalar=hgv[0][:, 1, dc, 0:1],
                    in1=xT[:, dc, q * 512:(q + 1) * 512], op0=ALU.mult, op1=ALU.add),
                    reads=[dps, d_gs[0]], writes=[dx[dc][q]])
        m.barrier(dummy[:])

    def pool_mixer():
        arena.reset(0)
        HW = 16 + NOWN + 16
        H = arena.take([NDC, HW], F32)
        d_H = [Dep() for _ in range(NDC)]
        carve_norm()
        stmp = arena.rot(2, [HW], F32)
        pooled = arena.rot(2, [2, NOWN], BF16)
        wpool_t = arena.take([4, 2, 256], BF16)
        poolw_t = arena.take([NDC, 2], F32)
        poolc_t = arena.take([NDC, 8], F32)
        mps_t = arena.take([NDC], F32)
        g8 = arena.take([ncc, 64], F32)
        hsel = arena.take([64], F32)
        hsend = arena.take([64], F32)
        d_pc = Dep()
        d_g8 = Dep()
        m.dma("gpsimd", wpool_t, wpool_d, writes=[d_pc])
        m.dma("sync", poolw_t, poolw_d, writes=[d_pc])
        m.dma("sync", poolc_t, poolc_d, writes=[d_pc])
        m.op("vector", lambda e: e.tensor_tensor(out=mps_t, in0=hgv[1][:, 1, :, 0], in1=pscT[:], op=ALU.mult),
             reads=[d_gs[1], d_small], writes=[d_pc])
        for dc in range(NDC):
            m.op("vector", lambda e, dc=dc: e.memset(H[:, dc, 0:16], 0.0), writes=[d_H[dc]])
            m.op("vector", lambda e, dc=dc: e.memset(H[:, dc, 16 + NOWN:], 0.0), writes=[d_H[dc]])
        for q in range(4):
            xsrc, dxs, n, col = own_seg(q)
            d_hq = Dep()
            modulate(1, 1, 0, xsrc, dxs, n, lambda dc, q=q: H[:, dc, 16 + q * 512:16 + (q + 1) * 512], d_hq)
            for dc in range(NDC):
                d_H[dc].w = d_hq.w
        m.op("vector", lambda e: e.tensor_copy(out=hsend.rearrange("p (c t) -> p c t", t=8),
                                               in_=H[:, :, 16 + NOWN - 8:16 + NOWN]), reads=d_H, writes=[d_g8])
        d_s2 = Dep()
        m.dma("sync", cc2_src, hsend, reads=[d_g8], writes=[d_s2])
        d_s2o = Dep()
        m.op("gpsimd", lambda e: e.collective_compute("AllGather", ALU.bypass, replica_groups=[list(range(ncc))],
                                                      ins=[cc2_src.opt()], outs=[cc2_dst.opt()]),
             reads=[d_s2], writes=[d_s2o])
        m.dma("sync", g8, cc2_dst.rearrange("(r p) f -> p r f", p=128), reads=[d_s2o], writes=[d_g8])
        m.op("vector", lambda e: e.tensor_scalar(out=hsel, in0=g8[:, 0, :], scalar1=pselT[:, 0:1], scalar2=None,
                                                 op0=ALU.mult), reads=[d_g8, d_small], writes=[d_g8])
        for r in range(1, ncc):
            m.op("vector", lambda e, r=r: e.scalar_tensor_tensor(out=hsel, in0=g8[:, r, :], scalar=pselT[:, r:r + 1],
                                                                 in1=hsel, op0=ALU.mult, op1=ALU.add),
                 reads=[d_g8, d_small], writes=[d_g8])
        hs3 = hsel.rearrange("p (c t) -> p c t", t=8)
        for j in range(8):
            m.op("vector", lambda e, j=j: e.tensor_copy(out=H[:, :, 16 + NOWN + j:16 + NOWN + j + 1],
                                                        in_=hs3[:, :, 7 - j:8 - j]), reads=[d_g8], writes=d_H)
        for gi, w in enumerate((2, 4, 8, 16)):
            pl, dpl = pooled.get()
            for kc in range(2):
                dc = 2 * gi + kc
                cur = H[:, dc, :]
                dcur = d_H[dc]
                span = 1
                while span < w:
                    st, dst_ = stmp.get()
                    m.op("vector", lambda e, st=st, cur=cur, span=span: e.tensor_tensor(
                        out=st[:, span:HW], in0=cur[:, span:HW], in1=cur[:, 0:HW - span], op=ALU.add),
                        reads=[dcur], writes=[dst_])
                    if span == 1:
                        m.op("vector", lambda e, st=st: e.memset(st[:, 0:1], 0.0), writes=[dst_])
                    cur, dcur, span = st, dst_, span * 2
                st, dst_ = stmp.get()
                o0 = 16 + w // 2 - 1
                m.op("vector", lambda e, st=st, cur=cur, dc=dc, o0=o0: e.tensor_scalar(
                    out=st[:, 0:NOWN], in0=cur[:, o0:o0 + NOWN], scalar1=poolw_t[:, dc, 0:1], scalar2=None,
                    op0=ALU.mult), reads=[dcur, d_pc], writes=[dst_])
                m.op("vector", lambda e, st=st, cur=cur, dc=dc, o0=o0: e.scalar_tensor_tensor(
                    out=st[:, 0:NOWN], in0=cur[:, o0 + 1:o0 + 1 + NOWN], scalar=poolw_t[:, dc, 1:2],
                    in1=st[:, 0:NOWN], op0=ALU.mult, op1=ALU.add), reads=[dcur, d_pc, dst_], writes=[dst_])
                m.op("vector", lambda e, st=st, dc=dc: e.tensor_tensor(out=st[:, 0:8], in0=st[:, 0:8],
                                                                       in1=poolc_t[:, dc, :], op=ALU.mult),
                     reads=[dst_, d_pc], writes=[dst_])
                m.op("vector", lambda e, st=st, dc=dc, pl=pl, kc=kc: e.tensor_tensor(
                    out=pl[:, kc, :], in0=st[:, 0:NOWN], in1=H[:, dc, 16:16 + NOWN], op=ALU.subtract),
                    reads=[dst_, d_H[dc]], writes=[dpl])
            for ec in range(2):
                dco = 2 * gi + ec
                for q in range(4):
                    ps, dps = banks.get()

                    def mm(e, ps=ps, pl=pl, gi=gi, ec=ec, q=q):
                        r_ = None
                        for kc in range(2):
                            r_ = e.matmul(ps, wpool_t[:, gi, kc, ec * 128:(ec + 1) * 128],
                                          pl[:, kc, q * 512:(q + 1) * 512], start=(kc == 0), stop=(kc == 1))
                        return r_

                    m.op("tensor", mm, reads=[dpl, d_pc], writes=[dps])
                    m.op("vector", lambda e, ps=ps, dco=dco, q=q: e.scalar_tensor_tensor(
                        out=xT[:, dco, q * 512:(q + 1) * 512], in0=ps, scalar=mps_t[:, dco:dco + 1],
                        in1=xT[:, dco, q * 512:(q + 1) * 512], op0=ALU.mult, op1=ALU.add),
                        reads=[dps, d_pc], writes=[dx[dco][q]])
        m.barrier(dummy[:])

    d_out = Dep()

    def final_norm():
        arena.reset(0)
        carve_norm()
        ost = arena.rot(3, [512], F32)
        for q in range(4):
            xsrc, dxs, n, col = own_seg(q)
            rt, drt = rms_rstd(xsrc, dxs, n)
            for dc in range(NDC):
                o, do = ost.get()
                m.op("vector", lambda e, dc=dc, o=o, q=q, rt=rt: e.tensor_tensor(out=o, in0=xT[:, dc, q * 512:(q + 1) * 512],
                                                                                 in1=rt, op=ALU.mult),
                     reads=[dx[dc][q], drt], writes=[do])
                m.op("vector", lambda e, dc=dc, o=o: e.tensor_scalar(out=o, in0=o, scalar1=fgT[:, dc:dc + 1],
                                                                     scalar2=None, op0=ALU.mult),
                     reads=[do, d_small], writes=[do])
                m.dma("sync", out_d[:, dc, q * 512:(q + 1) * 512], o, reads=[do], writes=[d_out])

    def dump_x():
        for dc in range(NDC):
            m.dma("sync", out_d[:, dc, :], xT[:, dc, :], reads=dx[dc], writes=[d_out])

    def snap(name):
        if not debug or name not in debug:
            return
        t = nc.dram_tensor("dbg_" + name, [128, NDC, NOWN], F32, kind="ExternalOutput").ap()
        dd = Dep()
        dbg_deps.append(dd)
        for dc in range(NDC):
            m.dma("sync", t[:, dc, :], xT[:, dc, :], reads=dx[dc], writes=[dd])

    carve_ffn()
    adaln(0)
    if stop != "mixonly":
        ffn_all(0, 0, 0, extra=True)
        adaln(1)
    snap("l0_ffn1")
    m.barrier(dummy[:])
    done = False
    if stop == "ffn1":
        dump_x()
        done = True
    if not done:
        if stop != "skipmix":
            mixer0()
        if stop in ("proj", "mix0", "mixonly"):
            dump_x()
            done = True
    if not done:
        snap("l0_mix")
        carve_ffn()
        ffn_all(0, 1, 2)
        snap("l0_ffn2")
        ffn_all(1, 0, 0)
        snap("l1_ffn1")
        m.barrier(dummy[:])
        pool_mixer()
        snap("l1_mix")
        carve_ffn()
        ffn_all(1, 1, 2)
        snap("l1_ffn2")
        m.barrier(dummy[:])
        final_norm()
    m.wait_all("sync", [d_out] + dbg_deps)
    m.build()
    return nc


def _chunkT(wcols):
    M = wcols.shape[1]
    t = np.zeros((NDC, 128, 128), np.float32)
    t[:, :, :M] = wcols.reshape(NDC, 128, M)
    return np.ascontiguousarray(t.transpose(1, 0, 2)).reshape(128, NDC * 128)


def _tokmajor_w(wcols):
    N = wcols.shape[1]
    return np.ascontiguousarray(wcols.reshape(NDC, 128, N).transpose(1, 0, 2)).reshape(128, NDC * N)


def prep_inputs(inp):
    f32 = np.float32
    x = np.asarray(inp["x"], f32)
    ctx = np.asarray(inp["ctx"], f32)
    c = np.asarray(inp["c"], f32)
    c_ctx = np.asarray(inp["c_ctx"], f32)
    shared = {}
    wm = np.asarray(inp["w_mod"], f32)
    shared["wmod"] = np.ascontiguousarray(
        wm.reshape(2, NDC, 128, 72, 128).transpose(0, 3, 2, 1, 4)).reshape(2, 72, 128, NDC * 128)
    bm = np.asarray(inp["b_mod"], f32)
    shared["bmod"] = np.ascontiguousarray(bm.reshape(2, 72, 128).transpose(2, 0, 1))
    shared["ng"] = np.ascontiguousarray(np.asarray(inp["norm_g"], f32).reshape(2, 3, NDC, 128).transpose(3, 0, 1, 2))
    shared["fg"] = np.ascontiguousarray(np.asarray(inp["final_g"], f32).reshape(NDC, 128).T)
    wi = np.stack([np.asarray(inp["ffn1_wi"], f32), np.asarray(inp["ffn2_wi"], f32)], 1)
    shared["wi"] = np.ascontiguousarray(
        wi.reshape(2, 2, NDC, 128, 2, NF, 128).transpose(0, 1, 5, 3, 2, 4, 6)).reshape(2, 2, NF, 128, NDC * 256)
    wo = np.stack([np.asarray(inp["ffn1_wo"], f32), np.asarray(inp["ffn2_wo"], f32)], 1)
    shared["wo"] = np.ascontiguousarray(
        wo.reshape(2, 2, NF, 128, NDC, 128).transpose(0, 1, 4, 3, 2, 5)).reshape(2, 2, NDC, 128, NF * 128)

    w_in = np.asarray(inp["w_in"], f32)[0]
    qa, ka, va = w_in[:, 0:512], w_in[:, 512:640], w_in[:, 640:768]
    qb, kb, vb, rb, zg = w_in[:, 768:1024], w_in[:, 1024:1280], w_in[:, 1280:1792], w_in[:, 1792:2304], w_in[:, 2304:]
    d_idx = np.arange(64)
    partner = np.where((d_idx % 32) < 16, d_idx + 16, d_idx - 16)

    def perm_heads(wc):
        nh = wc.shape[1] // 64
        return wc.reshape(1024, nh, 64)[:, :, partner].reshape(1024, nh * 64)

    qap, kap = perm_heads(qa), perm_heads(ka)

    def dup(wc, g):
        k_ = wc[:, g * 64:(g + 1) * 64]
        return np.concatenate([k_, k_], 1)

    shared["wva"] = _tokmajor_w(va)
    shared["wvb"] = _tokmajor_w(vb)
    shared["wkbt"] = _tokmajor_w(kb)
    hd = np.arange(128) % 64
    shared["mP"] = np.tile((np.arange(128)[:, None] >= np.arange(128)[None, :]).astype(f32), (1, 4))
    shared["mN"] = np.tile((np.arange(128)[:, None] <= np.arange(128)[None, :]).astype(f32), (1, 4))
    tri = np.stack([(np.arange(128)[:, None] <= np.arange(128)[None, :]).astype(f32),
                    (np.arange(128)[:, None] >= np.arange(128)[None, :]).astype(f32)], 1)
    shared["tri"] = np.ascontiguousarray(tri)
    shared["sinkb"] = np.ascontiguousarray(np.broadcast_to(np.asarray(inp["sink"], f32)[0][None, :], (128, 8)))
    shared["glag"] = np.ascontiguousarray(np.asarray(inp["gla_g"], f32)[0].reshape(128, 1))
    w_out = np.asarray(inp["w_out"], f32)[0]
    shared["wouta"] = np.ascontiguousarray(w_out[0:512].reshape(4, 128, D).transpose(1, 0, 2))
    shared["woutb"] = np.ascontiguousarray(w_out[512:].reshape(4, 128, D).transpose(1, 0, 2))
    wpool = np.asarray(inp["w_pool"], f32)[0]
    shared["wpool"] = np.ascontiguousarray(wpool.reshape(4, 2, 128, 256).transpose(2, 0, 1, 3))
    shared["pscale"] = np.ascontiguousarray(np.asarray(inp["pool_scale"], f32)[0].reshape(NDC, 128).T)
    wa2 = [np.asarray(inp["w_a2_f"], f32)[0], np.asarray(inp["w_a2_b"], f32)[0]]
    ba = [np.asarray(inp["b_a_f"], f32)[0], np.asarray(inp["b_a_b"], f32)[0]]
    freqs = (10000.0 ** (-np.arange(16, dtype=f32) / 16)).astype(f32)
    wins = (2, 4, 8, 16)

    in_maps = []
    for core in range(8):
        b, s = core // 2, core % 2
        u = np.arange(NOWN + NHALO)
        tok = u if s == 0 else (SEQ - 1 - u)
        cx = ctx[b] if s == 0 else ctx[b][::-1]
        allx = np.concatenate([x[b][tok], cx], 0)
        mp = dict(shared)
        mp["xin"] = np.ascontiguousarray(allx.reshape(NTOT, NDC, 128).transpose(2, 1, 0))
        cvv = np.stack([c[b], c_ctx], -1)
        mp["cv"] = np.ascontiguousarray(cvv.reshape(NDC, 128, 2).transpose(1, 0, 2))
        X, Y = (0, 1) if s == 0 else (1, 0)
        zgx, zgy = zg[:, X * 16:(X + 1) * 16], zg[:, Y * 16:(Y + 1) * 16]
        chunks = [qa[:, i * 128:(i + 1) * 128] for i in range(4)] + [qap[:, i * 128:(i + 1) * 128] for i in range(4)]
        chunks += [dup(ka, 0), dup(ka, 1), dup(kap, 0), dup(kap, 1)]
        chunks += [qb[:, 0:128], qb[:, 128:256], kb[:, 0:128], kb[:, 128:256]]
        chunks += [rb[:, i * 128:(i + 1) * 128] for i in range(4)] + [zgx, zgy]
        mp["wp"] = np.stack([_chunkT(cw) for cw in chunks], 0)
        mp["wa2"] = np.ascontiguousarray(np.stack([wa2[X], wa2[Y]], 1))
        mp["ba"] = np.ascontiguousarray(np.broadcast_to(np.stack([ba[X], ba[Y]], 0)[None], (128, 2, 256)))
        rows = (tok // 64).astype(f32)
        cols = (tok % 64).astype(f32)
        pos = np.where((hd < 32)[:, None], rows[None, :], cols[None, :]).astype(f32)
        ang = (pos * freqs[hd % 16][:, None]).astype(f32)
        sgn = np.where((hd % 32) < 16, -1.0, 1.0).astype(f32)[:, None]
        mp["cosT"] = np.cos(ang).astype(f32)
        mp["sinT"] = (np.sin(ang).astype(f32) * sgn).astype(f32)
        ps = np.zeros((128, 8), f32)
        ps[:, core ^ 1] = 1.0
        mp["psel"] = ps
        pw = np.zeros((128, NDC, 2), f32)
        pc = np.ones((128, NDC, 8), f32)
        for gi, w in enumerate(wins):
            for kc in range(2):
                dc = 2 * gi + kc
                pw[:, dc, s] = 1.0 / w
                for uu in range(8):
                    t_ = uu if s == 0 else SEQ - 1 - uu
                    lo = max(t_ - w // 2, 0)
                    hi = min(t_ + (w - w // 2), SEQ)
                    pc[:, dc, uu] = w / float(hi - lo)
        mp["poolw"] = pw
        mp["poolc"] = pc
        in_maps.append(mp)
    return in_maps


def assemble(res_list, key="out"):
    out = np.empty((4, SEQ, D), np.float32)
    for core in range(len(res_list)):
        b, s = core // 2, core % 2
        o = np.asarray(res_list[core][key])
        o = o.transpose(2, 1, 0).reshape(NOWN, D)
        if s == 0:
            out[b, 0:NOWN] = o
        else:
            out[b, NOWN:] = o[::-1]
    return out


def run(inp, stop=None):
    nc = build_program(stop)
    in_maps = prep_inputs(inp)
    res = run_bass_kernel_spmd(nc, in_maps, core_ids=list(range(8)))
    return assemble(res.results)


def kernel(**inputs):
    return run(inputs)
```
